# Optimizing a Trainium2 kernel written in Bass

```python
import jax, jax.numpy as jnp
from jax import lax
import numpy as np

D_MODEL = 1024
BATCH = 16
SEQ = 4096
DEPTH = 2

CHUNK = 64
SGU_BLOCK = 128
SGU_HEADS = 4
SGU_DIM = D_MODEL // 2
SGU_HEAD_DIM = SGU_DIM // SGU_HEADS
POOL_WINDOWS = (2, 4, 8, 16)
POOL_GROUPS = len(POOL_WINDOWS)
POOL_DIM = D_MODEL // 2
POOL_GROUP_DIM = POOL_DIM // POOL_GROUPS
IN_AB = 2 * SGU_DIM + POOL_DIM
MIX_AB = SGU_DIM + POOL_DIM
CONV_WIDTH = 3
CONV_DIM = D_MODEL
D_FF = ((-(-8 * D_MODEL // 3) + 255) // 256) * 256
N_EVEN = (DEPTH + 1) // 2
N_ODD = DEPTH // 2
EPS = 1e-6

kernel_name = "hybrid_sgu_pool_shortconv_trunk"


def rms_norm(x, g):
    xf = x.astype(jnp.float32)
    y = xf * lax.rsqrt(jnp.mean(xf * xf, axis=-1, keepdims=True) + EPS)
    return (y * g.astype(jnp.float32)).astype(x.dtype)


def layer_norm(x, g, b):
    xf = x.astype(jnp.float32)
    mu = jnp.mean(xf, axis=-1, keepdims=True)
    xc = xf - mu
    var = jnp.mean(xc * xc, axis=-1, keepdims=True)
    y = xc * lax.rsqrt(var + EPS)
    return (y * g.astype(jnp.float32) + b.astype(jnp.float32)).astype(x.dtype)


def sgu_mixer(z, ln_g, ln_b, ws, bs):
    bsz, s, _ = z.shape
    u = z[..., :SGU_DIM]
    v = layer_norm(z[..., SGU_DIM:], ln_g, ln_b)
    v = v.reshape(bsz, s // SGU_BLOCK, SGU_BLOCK, SGU_HEADS, SGU_HEAD_DIM)
    chunk_id = jnp.arange(SGU_BLOCK) // CHUNK
    mask = chunk_id[None, :] <= chunk_id[:, None]
    w = jnp.where(mask[None], ws, jnp.zeros_like(ws))
    vs = jnp.einsum('hij,bnjhd->bnihd', w, v) + bs[None, None, :, :, None]
    return u * vs.reshape(bsz, s, SGU_DIM)


def pool_mixer(p, pool_w, pool_b, pool_scale):
    s = p.shape[1]
    pf = p.astype(jnp.float32)
    cs = jnp.cumsum(pf, axis=1)
    t = jnp.arange(s)
    outs = []
    for g, win in enumerate(POOL_WINDOWS):
        sl = slice(g * POOL_GROUP_DIM, (g + 1) * POOL_GROUP_DIM)
        c = cs[..., sl]
        c_prev = jnp.pad(c[:, :-win], ((0, 0), (win, 0), (0, 0)))
        count = jnp.minimum(t + 1, win).astype(jnp.float32)[None, :, None]
        d = ((c - c_prev) / count - pf[..., sl]).astype(p.dtype)
        outs.append(d @ pool_w[g] + pool_b[g])
    return jnp.concatenate(outs, axis=-1) * pool_scale


def short_conv_mixer(h, conv_w, conv_b):
    s = h.shape[1]
    b_gate = h[..., :CONV_DIM]
    c_gate = h[..., CONV_DIM:2 * CONV_DIM]
    hv = h[..., 2 * CONV_DIM:]
    q = c_gate * hv
    qp = jnp.pad(q, ((0, 0), (CONV_WIDTH - 1, 0), (0, 0)))
    y = conv_b + sum(conv_w[k] * qp[:, k:k + s] for k in range(CONV_WIDTH))
    return b_gate * y


def swiglu(x, w_gate, w_up, w_down):
    return (jax.nn.silu(x @ w_gate) * (x @ w_up)) @ w_down


def setup_inputs(seed: int = 0) -> dict:
    key = jax.random.key(seed)
    ks = jax.random.split(key, 32)
    f32 = jnp.float32

    def nrm(k, shape, fan_in):
        return jax.random.normal(k, shape, f32) * (fan_in ** -0.5)

    def gain(k, shape):
        return 1.0 + 0.05 * jax.random.normal(k, shape, f32)

    def small(k, shape):
        return 0.02 * jax.random.normal(k, shape, f32)

    return {
        "x": jax.random.normal(ks[0], (BATCH, SEQ, D_MODEL), f32),
        "even_norm": gain(ks[1], (N_EVEN, D_MODEL)),
        "even_w_in": nrm(ks[2], (N_EVEN, D_MODEL, IN_AB), D_MODEL),
        "even_sgu_ln_g": gain(ks[3], (N_EVEN, SGU_DIM)),
        "even_sgu_ln_b": small(ks[4], (N_EVEN, SGU_DIM)),
        "even_sgu_ws": nrm(ks[5], (N_EVEN, SGU_HEADS, SGU_BLOCK, SGU_BLOCK), SGU_BLOCK),
        "even_sgu_bs": 1.0 + 0.1 * jax.random.normal(ks[6], (N_EVEN, SGU_BLOCK, SGU_HEADS), f32),
        "even_pool_w": nrm(ks[7], (N_EVEN, POOL_GROUPS, POOL_GROUP_DIM, POOL_GROUP_DIM), POOL_GROUP_DIM),
        "even_pool_b": small(ks[8], (N_EVEN, POOL_GROUPS, POOL_GROUP_DIM)),
        "even_pool_scale": gain(ks[9], (N_EVEN, POOL_DIM)),
        "even_w_out": nrm(ks[10], (N_EVEN, MIX_AB, D_MODEL), MIX_AB),
        "odd_norm": gain(ks[11], (N_ODD, D_MODEL)),
        "odd_w_in": nrm(ks[12], (N_ODD, D_MODEL, 3 * CONV_DIM), D_MODEL),
        "odd_conv_w": nrm(ks[13], (N_ODD, CONV_WIDTH, CONV_DIM), CONV_WIDTH),
        "odd_conv_b": small(ks[14], (N_ODD, CONV_DIM)),
        "odd_w_out": nrm(ks[15], (N_ODD, CONV_DIM, D_MODEL), CONV_DIM),
        "ffn_norm": gain(ks[16], (DEPTH, D_MODEL)),
        "ffn_w_gate": nrm(ks[17], (DEPTH, D_MODEL, D_FF), D_MODEL),
        "ffn_w_up": nrm(ks[18], (DEPTH, D_MODEL, D_FF), D_MODEL),
        "ffn_w_down": nrm(ks[19], (DEPTH, D_FF, D_MODEL), D_FF),
        "final_norm": gain(ks[20], (D_MODEL,)),
    }


def reference(x, even_norm, even_w_in, even_sgu_ln_g, even_sgu_ln_b, even_sgu_ws,
              even_sgu_bs, even_pool_w, even_pool_b, even_pool_scale, even_w_out,
              odd_norm, odd_w_in, odd_conv_w, odd_conv_b, odd_w_out,
              ffn_norm, ffn_w_gate, ffn_w_up, ffn_w_down, final_norm):
    for layer in range(DEPTH):
        i = layer // 2
        if layer % 2 == 0:
            hn = rms_norm(x, even_norm[i])
            h = hn @ even_w_in[i]
            z = jax.nn.gelu(h[..., :2 * SGU_DIM], approximate=False)
            a_out = sgu_mixer(z, even_sgu_ln_g[i], even_sgu_ln_b[i],
                              even_sgu_ws[i], even_sgu_bs[i])
            b_out = pool_mixer(h[..., 2 * SGU_DIM:], even_pool_w[i],
                               even_pool_b[i], even_pool_scale[i])
            mix = jnp.concatenate([a_out, b_out], axis=-1) @ even_w_out[i]
        else:
            hn = rms_norm(x, odd_norm[i])
            h = hn @ odd_w_in[i]
            mix = short_conv_mixer(h, odd_conv_w[i], odd_conv_b[i]) @ odd_w_out[i]
        x = x + mix
        hn = rms_norm(x, ffn_norm[layer])
        x = x + swiglu(hn, ffn_w_gate[layer], ffn_w_up[layer], ffn_w_down[layer])
    return rms_norm(x, final_norm)
```

```python
import contextlib
import numpy as np
import concourse.bass as bass
import concourse.mybir as mybir
from concourse.bass_utils import run_bass_kernel_spmd

F32 = mybir.dt.float32
BF16 = mybir.dt.bfloat16
I32 = mybir.dt.int32
AF = mybir.ActivationFunctionType
ALU = mybir.AluOpType

D = 1024
DFF = 2816
NF = 22
T = 1024
NB = 8
NCORES = 8
TOK_CORE = 8192
SEQ = 4096
PASS_PER_SEQ = SEQ // T
EPS = 1e-6
POOL_WINDOWS = (2, 4, 8, 16)
RING_SLOTS = 4
PUMP_EVERY = 2
RSQRT_ENG = "dve"
STAGED = True
DOWN_F = ((0, 6), (6, 14), (14, 22))
STRICT_SAME_ENGINE = False
RING_ELEMS = 4096

C_BSB = 0
C_FING = C_BSB + 512
C_VEC = C_FING + 1024
V_LNG, V_LNB, V_PB, V_PS = 0, 4, 8, 12
V_CW = 16
V_CB = 40
V_N_EVEN, V_N_ODD, V_N_FFN0, V_N_FFN1 = 48, 56, 64, 72
V_EPS, V_MHALF, V_1P5, V_MAGIC = 80, 81, 82, 83
NVEC = 84
NCF = C_VEC + NVEC
CB_ID = 0
CB_B = 128
NCB = CB_B + 4 * 3 * 128


class Region:
    __slots__ = ("name", "w", "r", "dsem", "dkey", "dcnt", "pend")

    def __init__(self, name):
        self.name = name
        self.w = None
        self.r = {}
        self.dsem = None
        self.dkey = None
        self.dcnt = 0
        self.pend = 0


class Ctx:
    def __init__(self, nc, stack):
        self.nc = nc
        self.stack = stack
        self.eng = {"pe": nc.tensor, "act": nc.scalar, "dve": nc.vector, "pool": nc.gpsimd, "sp": nc.sync}
        self.semh = {}
        self.cnt = {}
        self.seen = {e: {} for e in self.eng}
        self.deferred = []
        self.in_flush = False
        self.pe_ops = 0
        for e in self.eng:
            self.semh[e] = stack.enter_context(nc.semaphore("sem_" + e))
            self.cnt[e] = 0

    def dma_region(self, name):
        r = Region(name)
        r.dkey = "dma_" + name
        r.dsem = self.stack.enter_context(self.nc.semaphore("sd_" + name))
        self.semh[r.dkey] = r.dsem
        return r

    def defer_tick(self, items):
        if not self.deferred:
            self.pe_ops = 0
        for _, touches in items:
            for r in touches:
                r.pend += 1
        self.deferred.append(items)

    def pump_tick(self):
        if not self.deferred:
            return
        items = self.deferred.pop(0)
        self.in_flush = True
        for fn, touches in items:
            for r in touches:
                r.pend -= 1
            fn()
        self.in_flush = False

    def flush_all(self):
        while self.deferred:
            self.pump_tick()

    def _sync_deferred(self, regs):
        if self.in_flush:
            return
        while self.deferred and any(r.pend for r in regs):
            self.pump_tick()

    def _waits(self, me, reads, writes):
        need = {}

        def add(h, raw):
            if h is None:
                return
            key, val = h
            if key == me and not raw and (me == "pe" or not STRICT_SAME_ENGINE):
                return
            if val > need.get(key, 0):
                need[key] = val
        for r in reads:
            add(r.w, True)
        for w in writes:
            add(w.w, False)
            for key, val in w.r.items():
                add((key, val), False)
        seen = self.seen[me]
        for key, val in need.items():
            if seen.get(key, 0) >= val:
                continue
            self.eng[me].wait_ge(self.semh[key], val)
            seen[key] = val

    def op(self, me, fn, reads=(), writes=()):
        self._sync_deferred(list(reads) + list(writes))
        self._waits(me, reads, writes)
        ins = fn(self.eng[me])
        self.cnt[me] += 1
        ins.then_inc(self.semh[me], 1)
        h = (me, self.cnt[me])
        for r in reads:
            if h[1] > r.r.get(me, 0):
                r.r[me] = h[1]
        for w in writes:
            w.w = h
            w.r = {}
        if me == "pe" and not self.in_flush and self.deferred:
            self.pe_ops += 1
            if self.pe_ops % PUMP_EVERY == 0:
                self.pump_tick()
        return h

    def dma(self, q, out_ap, in_ap, owner, store=False, reads=(), writes=()):
        if store:
            reads = list(reads) + [owner]
        else:
            writes = list(writes) + [owner]
        self._sync_deferred(list(reads) + list(writes))
        self._waits(q, reads, writes)
        outs = out_ap if isinstance(out_ap, (list, tuple)) else [out_ap]
        ins_ = in_ap if isinstance(in_ap, (list, tuple)) else [in_ap]
        for o, i in zip(outs, ins_):
            ins = self.eng[q].dma_start(out=o, in_=i)
            owner.dcnt += 16
            ins.then_inc(owner.dsem, 16)
        h = (owner.dkey, owner.dcnt)
        for r in reads:
            if h[1] > r.r.get(h[0], 0):
                r.r[h[0]] = h[1]
        for w in writes:
            w.w = h
            w.r = {}
        return h


class Rot:
    def __init__(self, aps, name):
        self.aps = aps
        self.regs = [Region("%s%d" % (name, i)) for i in range(len(aps))]
        self.i = 0

    def get(self):
        i = self.i
        self.i = (i + 1) % len(self.aps)
        return self.aps[i], self.regs[i]


def build_nc(layers=(0, 1), final=True, npass=TOK_CORE // T, dbg=""):
    nc = bass.Bass("TRN2", target_bir_lowering=False)
    dt_in = lambda name, shape: nc.dram_tensor(name, list(shape), F32, kind="ExternalInput").ap()
    x_d = dt_in("x", (TOK_CORE, D))
    cstf_d = dt_in("cstf", (128, NCF))
    cstb_d = dt_in("cstb", (128, NCB))
    wst_d = dt_in("wst", (128, 4, 128))
    pw_d = dt_in("pw", (128, 4, 128))
    win0_d = dt_in("win0", (D, 1536))
    wout0_d = dt_in("wout0", (D, D))
    win1_d = dt_in("win1", (D, 3 * D))
    wout1_d = dt_in("wout1", (D, D))
    wg_d = [dt_in("wg%d" % l, (D, DFF)) for l in range(2)]
    wu_d = [dt_in("wu%d" % l, (D, DFF)) for l in range(2)]
    wd_d = [dt_in("wd%d" % l, (DFF, D)) for l in range(2)]
    y_d = nc.dram_tensor("y", [TOK_CORE, D], F32, kind="ExternalOutput").ap()

    with contextlib.ExitStack() as stack:
        cx = Ctx(nc, stack)
        sb = lambda name, shape, dt: stack.enter_context(nc.sbuf_tensor(name, list(shape), dt))
        x_tm = sb("x_tm", (128, NB, D), F32)
        xr = [cx.dma_region("x%d" % b) for b in range(NB)]
        junk = sb("junk", (128, D), BF16)
        junk_r = Region("junk")
        xn_t = [sb("xn%d" % i, (128, D), BF16) for i in range(4)]
        xn_rot = Rot([t[:] for t in xn_t], "xn")
        st_t = [sb("st%d" % i, (128, 16), F32) for i in range(10)]
        st_rot = Rot(st_t, "st")
        hn = sb("hn", (128, 8, T), BF16)
        hnr = [Region("hn%d" % b) for b in range(NB)]
        mixin = sb("mixin", (128, 8, T), BF16)
        mx = [[Region("mx%d_%d" % (k, s)) for s in range(2)] for k in range(8)]
        hbuf = sb("hbuf", (128, NF, T), BF16)
        hr = [[Region("h%d_%d" % (f, s)) for s in range(2)] for f in range(NF)]
        vn_ap = [hbuf[:, b, 0:512] for b in range(NB)]
        vnr = [hr[b][0] for b in range(NB)]
        ptm_ap = [hbuf[:, 8 + b, 0:512] for b in range(NB)]
        ptr = [hr[8 + b][0] for b in range(NB)]
        stg2 = sb("stg2", (128, 4, D), F32)
        stg1 = mixin[:].rearrange("p k t -> p (k t)").bitcast(F32).rearrange("p (j d) -> p j d", j=4)
        stg_ap = [stg1[:, j, :] for j in range(4)] + [stg2[:, j, :] for j in range(4)]
        stg_r2 = [cx.dma_region("stg%d" % j) for j in range(2)]
        stg_r = [stg_r2[j // 4] for j in range(8)]
        stg_alias_all = [[mx[k][s_] for k in range(8) for s_ in range(2)], []]
        stg_alias = [[mx[2 * j][0], mx[2 * j][1], mx[2 * j + 1][0], mx[2 * j + 1][1]] for j in range(4)] + [[] for _ in range(4)]
        p_halo = sb("p_halo", (128, 512), BF16)
        p_halo_r = Region("p_halo")
        fr_t = [sb("fr%d" % i, (128, 514), F32) for i in range(6)]
        frot = Rot(fr_t, "fr")
        br_t = [sb("br%d" % i, (128, 512), BF16) for i in range(3)]
        brot = Rot([t[:] for t in br_t], "br")
        bn_t = [sb("bn%d" % i, (128, 6), F32) for i in range(4)]
        bn_rot = Rot([t[:] for t in bn_t], "bn")
        mv_t = [sb("mv%d" % i, (128, 8), F32) for i in range(2)]
        mv_rot = Rot([t[:] for t in mv_t], "mv")
        outb_t = [sb("outb%d" % i, (128, D), F32) for i in range(2)]
        outb_r = [cx.dma_region("outb%d" % i) for i in range(2)]
        qtail = sb("qtail", (128, 8, 2), F32)
        qtail_r = [Region("qt%d" % c) for c in range(8)]
        ring_t = [sb("ring%d" % i, (128, RING_ELEMS), BF16) for i in range(RING_SLOTS)]
        ring_r = [cx.dma_region("ring%d" % i) for i in range(RING_SLOTS)]
        cstf = sb("cstf_s", (128, NCF), F32)
        cstf_r = cx.dma_region("cstf")
        cstb = sb("cstb_s", (128, NCB), BF16)
        cstb_r = cx.dma_region("cstb")
        wtb = sb("wtb", (128, 4, 128), BF16)
        wtb_r = cstb_r
        pwb = sb("pwb", (128, 4, 128), BF16)
        pwb_r = cstb_r
        ones_b = sb("ones_b", (128, 128), BF16)
        ones_r = Region("ones")
        Ct = sb("Ct", (128, 4, 128), F32)
        Ct_r = Region("Ct")
        ps_t = [stack.enter_context(nc.psum_tensor("ps%d" % i, [128, 512], F32)) for i in range(8)]
        ps_r = [Region("ps%d" % i) for i in range(8)]
        bank_i = [0]

        def next_bank():
            i = bank_i[0]
            bank_i[0] = (i + 1) % 8
            return ps_t[i], ps_r[i]
        bank_override = [None]

        def pick_bank(i):
            return bank_override[0](i) if bank_override[0] else next_bank()

        vec = lambda col, n=1: cstf[:, C_VEC + col:C_VEC + col + n]
        ident_b = cstb[:, CB_ID:CB_ID + 128]

        def Bmat(g, kind):
            o = CB_B + (g * 3 + kind) * 128
            return cstb[:, o:o + 128]

        kview = lambda w: w.rearrange("(k p) n -> p k n", p=128)
        slabs = []

        def slab_list():
            lst = []
            for l in layers:
                if l == 0:
                    v = kview(win0_d)
                    for j, nm in enumerate(("in0_u", "in0_v", "in0_p")):
                        lst.append((nm, v[:, :, j * 512:(j + 1) * 512], "k512"))
                    v = kview(wout0_d)
                    for j in range(2):
                        lst.append(("out0_%d" % j, v[:, :, j * 512:(j + 1) * 512], "k512"))
                else:
                    v = win1_d.rearrange("(k p) (s c n) -> p k s c n", p=128, s=3, c=8)
                    for c in range(8):
                        lst.append(("in1_%d" % c, [v[:, :, sec, c, :] for sec in range(3)], "k3x128"))
                    v = kview(wout1_d)
                    for j in range(2):
                        lst.append(("out1_%d" % j, v[:, :, j * 512:(j + 1) * 512], "k512"))
                vg, vu = kview(wg_d[l]), kview(wu_d[l])
                for c in range(11):
                    lst.append(("gu%d_%d" % (l, c), [vg[:, :, c * 256:(c + 1) * 256], vu[:, :, c * 256:(c + 1) * 256]], "gu"))
                vd = wd_d[l].rearrange("(f p) n -> p f n", p=128)
                for half in range(2):
                    for fg in range(3):
                        f0, f1 = DOWN_F[fg]
                        lst.append(("d%d_%d_%d" % (l, half, fg),
                                    vd[:, f0:f1, half * 512:(half + 1) * 512], "f%d" % (f1 - f0)))
            return lst

        per_pass = slab_list()
        all_slabs = per_pass * npass
        st_ = {"issued": 0, "next": 0}

        def slab_view(slot, kind):
            t = ring_t[slot]
            if kind in ("k512", "gu"):
                return t[:, 0:4096].rearrange("p (k n) -> p k n", k=8)
            if kind == "k256":
                return t[:, 0:2048].rearrange("p (k n) -> p k n", k=8)
            if kind == "k3x128":
                return t[:, 0:3072].rearrange("p (k s n) -> p k s n", k=8, s=3)
            if kind == "f8":
                return t[:, 0:4096].rearrange("p (f n) -> p f n", f=8)
            if kind == "f6":
                return t[:, 0:3072].rearrange("p (f n) -> p f n", f=6)
            raise ValueError(kind)

        def issue_upto(j):
            while st_["issued"] <= j and st_["issued"] < len(all_slabs):
                i = st_["issued"]
                nm, src, kind = all_slabs[i]
                slot = i % RING_SLOTS
                v = slab_view(slot, kind)
                if kind == "gu":
                    cx.dma("pool", [v[:, :, 0:256], v[:, :, 256:512]], src, ring_r[slot])
                elif kind == "k3x128":
                    cx.dma("pool", [v[:, :, sec, :] for sec in range(3)], src, ring_r[slot])
                else:
                    cx.dma("pool", v, src, ring_r[slot])
                st_["issued"] += 1

        def next_slab(prefix, hold=0):
            j = st_["next"]
            st_["next"] += 1
            nm, src, kind = all_slabs[j]
            assert nm.startswith(prefix), (nm, prefix)
            issue_upto(j - hold + RING_SLOTS - 1)
            slot = j % RING_SLOTS
            return slab_view(slot, kind), ring_r[slot]

        cx.dma("sp", cstf[:], cstf_d, cstf_r)
        cx.dma("pool", [cstb[:], wtb[:], pwb[:]], [cstb_d, wst_d, pw_d], cstb_r)
        cx.op("dve", lambda e: e.memset(ones_b[:], 1.0), writes=[ones_r])
        cx.op("dve", lambda e: e.memset(wtb[64:128, :, 0:64], 0.0), writes=[wtb_r])
        cx.op("dve", lambda e: e.memset(qtail[:], 0.0), writes=qtail_r)
        cx.op("dve", lambda e: e.memset(p_halo[:], 0.0), writes=[p_halo_r])
        if 0 in layers:
            for h in range(4):
                pt, pr = next_bank()
                cx.op("pe", lambda e: e.matmul(pt[:, 0:128], lhsT=ones_b[:], rhs=wtb[:, h, :], start=True, stop=True),
                      reads=[ones_r, wtb_r], writes=[pr])
                cx.op("dve", lambda e: e.scalar_tensor_tensor(
                    out=Ct[:, h, :], in0=pt[:, 0:128], scalar=vec(V_LNB + h),
                    in1=cstf[:, C_BSB + h * 128:C_BSB + (h + 1) * 128], op0=ALU.mult, op1=ALU.add),
                    reads=[pr, cstf_r], writes=[Ct_r])

        def rsqrt_batch(stt, str_, n, iters=2):
            a = stt[:, 0:n]
            y = stt[:, 4:4 + n]
            t = stt[:, 8:8 + n]
            t0 = stt[:, 12:12 + n]
            D_ = lambda fn: cx.op("dve", fn, reads=[str_], writes=[str_])
            D_(lambda e: e.tensor_scalar(out=a, in0=a, scalar1=EPS, scalar2=None, op0=ALU.add))
            D_(lambda e: e.tensor_copy(out=t0, in_=a.bitcast(I32)))
            D_(lambda e: e.tensor_scalar(out=y.bitcast(I32), in0=t0, scalar1=-0.5, scalar2=1597463007.0,
                                         op0=ALU.mult, op1=ALU.add))
            for _ in range(iters):
                D_(lambda e: e.tensor_tensor(out=t, in0=a, in1=y, op=ALU.mult))
                D_(lambda e: e.tensor_tensor(out=t, in0=t, in1=y, op=ALU.mult))
                D_(lambda e: e.tensor_scalar(out=t, in0=t, scalar1=-0.5, scalar2=1.5, op0=ALU.mult, op1=ALU.add))
                D_(lambda e: e.tensor_tensor(out=y, in0=y, in1=t, op=ALU.mult))

        def rsqrt_pool(sl, sreg):
            ms, t0, t, ya, yb = (sl[:, i:i + 1] for i in range(5))
            P_ = lambda fn: cx.op(RSQRT_ENG, fn, reads=[sreg, cstf_r], writes=[sreg])
            P_(lambda e: e.tensor_copy(out=t0, in_=ms.bitcast(I32)))
            P_(lambda e: e.tensor_scalar(out=ya.bitcast(I32), in0=t0, scalar1=vec(V_MHALF), scalar2=vec(V_MAGIC),
                                         op0=ALU.mult, op1=ALU.add))
            for (y0, y1) in ((ya, yb), (yb, ya)):
                P_(lambda e: e.tensor_scalar(out=t, in0=ms, scalar1=vec(V_EPS), scalar2=y0, op0=ALU.add, op1=ALU.mult))
                P_(lambda e: e.tensor_scalar(out=t, in0=t, scalar1=y0, scalar2=vec(V_MHALF), op0=ALU.mult, op1=ALU.mult))
                P_(lambda e: e.tensor_scalar(out=y1, in0=t, scalar1=vec(V_1P5), scalar2=y0, op0=ALU.add, op1=ALU.mult))

        SRC_X = (lambda i: x_tm[:, i, :], lambda i: [xr[i]])
        SRC_STG = (lambda i: stg_ap[i], lambda i: [stg_r[i]] + stg_alias[i])

        def stats_stages(lag0, store, src=None, lagR=0):
            cur = {}
            src_ap, src_regs = src or SRC_X

            def S1(i):
                if i % 4 == 0:
                    cur["t"] = st_rot.get()
                stt, str_ = cur["t"]
                store[i] = (stt, str_, 4 + i % 4)
                cx.op("act", lambda e: e.activation(out=junk[:], in_=src_ap(i), func=AF.Square,
                                                    scale=1.0 / 32.0, accum_out=stt[:, i % 4:i % 4 + 1]),
                      reads=src_regs(i), writes=[junk_r, str_])

            def R(i):
                if i % 4 == 3:
                    stt, str_, _ = store[i]
                    rsqrt_batch(stt, str_, 4)
            return [(lag0, S1, lambda i: src_regs(i)), (lag0 + lagR, R, lambda i: src_regs(i))]

        def norm_stages(gcol, lag0=0, src=None, stats=None, lagA=None, lagB=None):
            stt = stats if stats is not None else {}
            xbuf = {}
            src_ap, src_regs = src or SRC_X

            def A2(i):
                sl, sreg, col = stt[i]
                xa, xreg = xn_rot.get()
                xbuf[i] = (xa, xreg)
                cx.op("act", lambda e: e.mul(out=xa, in_=src_ap(i), mul=sl[:, col:col + 1]),
                      reads=src_regs(i) + [sreg], writes=[xreg])

            def B(i):
                xa, xreg = xbuf[i]
                pt, pr = pick_bank(i)
                pb = pt[:].bitcast(BF16)

                def tr(e):
                    ins = None
                    for k in range(8):
                        ins = e.transpose(out=pb[:, k * 128:(k + 1) * 128], in_=xa[:, k * 128:(k + 1) * 128],
                                          identity=ident_b)
                    return ins
                cx.op("pe", tr, reads=[xreg, cstb_r], writes=[pr])
                cx.op("dve", lambda e: e.tensor_tensor(
                    out=hn[:, :, i * 128:(i + 1) * 128],
                    in0=pb.rearrange("p (k n) -> p k n", k=8),
                    in1=vec(gcol, 8).unsqueeze(2).to_broadcast([128, 8, 128]), op=ALU.mult),
                    reads=[pr, cstf_r], writes=[hnr[i]])
            st = [] if stats is not None else stats_stages(lag0, stt, src)
            if lagA is None:
                tA = lambda i: lag0 + 4 * (i // 4) + 4 + (i % 4) // 2
                tB = lambda i: lag0 + 4 * (i // 4) + 5 + (i % 4) // 2
            else:
                tA, tB = lag0 + lagA, lag0 + lagB
            return st + [(tB, B, lambda i: src_regs(i) + [hnr[i]]), (tA, A2, lambda i: src_regs(i))]

        def pipeline(n, tick, stages):
            cx.flush_all()
            sched = {}
            for si, (lag, fn, touches) in enumerate(stages):
                for i in range(n):
                    t = lag(i) if callable(lag) else i + lag
                    sched.setdefault(t, []).append((si, i, fn, touches))
            tmax = max(list(sched) + [n - 1])
            for t in range(tmax + 1):
                items = sorted(sched.get(t, []), key=lambda it: (it[0], it[1]))
                if t < n:
                    if tick is not None:
                        tick(t)
                    for _, i, fn, _t in items:
                        fn(i)
                elif items:
                    cx.defer_tick([(lambda fn=fn, i=i: fn(i), touches(i)) for _, i, fn, touches in items])

        def mm_group(pt, pairs):
            def fn(e):
                ins = None
                n = len(pairs)
                for i, (l, r) in enumerate(pairs):
                    ins = e.matmul(pt, lhsT=l, rhs=r, start=(i == 0), stop=(i == n - 1))
                return ins
            return fn

        def residual_phase(prefix, src, src_regs_fn, tail):
            sl0 = next_slab(prefix)
            sl1 = next_slab(prefix, hold=1)

            def tick(blk):
                for half, (sl, slr) in enumerate((sl0, sl1)):
                    pt, pr = next_bank()
                    cx.op("pe", mm_group(pt[:], [(src[:, k, blk * 128:(blk + 1) * 128], sl[:, k, :]) for k in range(8)]),
                          reads=[slr] + src_regs_fn(blk), writes=[pr])
                    xs = x_tm[:, blk, half * 512:(half + 1) * 512]
                    cx.op("dve", lambda e: e.tensor_tensor(out=xs, in0=pt[:], in1=xs, op=ALU.add),
                          reads=[pr, xr[blk]], writes=[xr[blk]])
            pipeline(NB, tick, tail)

        def ffn_phase(l, tail, pre_tail=None):
            def gu_groups(c, sl, slr, s):
                for fi in range(2):
                    f = 2 * c + fi
                    rhs = [hn[:, k, s * 512:(s + 1) * 512] for k in range(8)]
                    hreads = [hnr[4 * s + i] for i in range(4)]
                    pa, par = next_bank()
                    cx.op("pe", mm_group(pa[:], [(sl[:, k, fi * 128:(fi + 1) * 128], rhs[k]) for k in range(8)]),
                          reads=[slr] + hreads, writes=[par])
                    pb_, pbr = next_bank()
                    cx.op("pe", mm_group(pb_[:], [(sl[:, k, 256 + fi * 128:256 + (fi + 1) * 128], rhs[k]) for k in range(8)]),
                          reads=[slr] + hreads, writes=[pbr])
                    ft, ftr = frot.get()
                    cx.op("act", lambda e: e.activation(out=ft[:, 0:512], in_=pa[:], func=AF.Silu),
                          reads=[par], writes=[ftr])
                    cx.op("dve", lambda e: e.tensor_tensor(out=hbuf[:, f, s * 512:(s + 1) * 512], in0=pb_[:],
                                                           in1=ft[:, 0:512], op=ALU.mult),
                          reads=[pbr, ftr], writes=[hr[f][s]])
            s0 = next_slab("gu%d_0" % l)
            s1 = next_slab("gu%d_1" % l, hold=1)
            gu_groups(0, s0[0], s0[1], 0)
            gu_groups(1, s1[0], s1[1], 0)
            gu_groups(0, s0[0], s0[1], 1)
            gu_groups(1, s1[0], s1[1], 1)
            for c in range(2, 11):
                sl, slr = next_slab("gu%d_%d" % (l, c))
                for s in range(2):
                    gu_groups(c, sl, slr, s)
            def down_group(blk, sl, slr, f0, f1):
                pt, pr = ps_t[blk], ps_r[blk]
                s = blk // 4

                def fn(e):
                    ins = None
                    for f in range(f0, f1):
                        ins = e.matmul(pt[:], lhsT=hbuf[:, f, blk * 128:(blk + 1) * 128], rhs=sl[:, f - f0, :],
                                       start=(f == 0), stop=(f == NF - 1))
                    return ins
                cx.op("pe", fn, reads=[slr] + [hr[f][s] for f in range(f0, f1)], writes=[pr])

            def down_add(blk, half):
                pt, pr = ps_t[blk], ps_r[blk]
                xs = x_tm[:, blk, half * 512:(half + 1) * 512]
                cx.op("dve", lambda e: e.tensor_tensor(out=xs, in0=pt[:], in1=xs, op=ALU.add),
                      reads=[pr, xr[blk]], writes=[xr[blk]])

            for fg in range(3):
                sl, slr = next_slab("d%d_0_%d" % (l, fg))
                for blk in range(NB):
                    down_group(blk, sl, slr, *DOWN_F[fg])
                    if fg == 2:
                        down_add(blk, 0)
            sl, slr = next_slab("d%d_1_0" % l)

            def tick_a(blk, sl=sl, slr=slr):
                down_group(blk, sl, slr, *DOWN_F[0])
            if pre_tail:
                pipeline(NB, tick_a, pre_tail)
            else:
                for blk in range(NB):
                    tick_a(blk)
            sa = next_slab("d%d_1_1" % l)
            sb_ = next_slab("d%d_1_2" % l, hold=1)

            def tick_b(blk):
                down_group(blk, sa[0], sa[1], *DOWN_F[1])
                down_group(blk, sb_[0], sb_[1], *DOWN_F[2])
                down_add(blk, 1)
            bank_override[0] = lambda i: (ps_t[i], ps_r[i])
            pipeline(NB, tick_b, tail)
            bank_override[0] = None
            bank_i[0] = 0

        def layer0(first_of_seq, tail):
            slu, slur = next_slab("in0_u")
            slv, slvr = next_slab("in0_v", hold=1)

            def u_part(s):
                for o in range(4):
                    pt, pr = next_bank()
                    cx.op("pe", mm_group(pt[:], [(slu[:, k, o * 128:(o + 1) * 128], hn[:, k, s * 512:(s + 1) * 512])
                                                 for k in range(8)]),
                          reads=[slur] + [hnr[4 * s + i] for i in range(4)], writes=[pr])
                    cx.op("act", lambda e: e.activation(out=mixin[:, o, s * 512:(s + 1) * 512], in_=pt[:], func=AF.Gelu),
                          reads=[pr], writes=[mx[o][s]])

            def v_part(s):
                stt, str_ = st_rot.get()
                mv, mvr = mv_rot.get()
                v32s = []
                for b4 in range(4):
                    blk = 4 * s + b4
                    pt, pr = next_bank()
                    cx.op("pe", mm_group(pt[:], [(hn[:, k, blk * 128:(blk + 1) * 128], slv[:, k, :]) for k in range(8)]),
                          reads=[slvr, hnr[blk]], writes=[pr])
                    ft, ftr = frot.get()
                    v32s.append((ft, ftr))
                    cx.op("act", lambda e: e.activation(out=ft[:, 0:512], in_=pt[:], func=AF.Gelu),
                          reads=[pr], writes=[ftr])
                    bn, bnr = bn_rot.get()
                    cx.op("dve", lambda e: e.bn_stats(out=bn, in_=ft[:, 0:512]), reads=[ftr], writes=[bnr])
                    cx.op("dve", lambda e: e.bn_aggr(out=mv[:, 2 * b4:2 * b4 + 2], in_=bn), reads=[bnr], writes=[mvr])
                cx.op("dve", lambda e: e.tensor_copy(out=stt[:, 0:4], in_=mv.rearrange("p (b t) -> p b t", t=2)[:, :, 1]),
                      reads=[mvr], writes=[str_])
                rsqrt_batch(stt, str_, 4, iters=3)
                for b4 in range(4):
                    blk = 4 * s + b4
                    ft, ftr = v32s[b4]
                    cx.op("dve", lambda e: e.tensor_scalar(out=vn_ap[blk], in0=ft[:, 0:512], scalar1=mv[:, 2 * b4:2 * b4 + 1],
                                                           scalar2=stt[:, 4 + b4:5 + b4], op0=ALU.subtract, op1=ALU.mult),
                          reads=[ftr, mvr, str_], writes=[vnr[blk]])
            u_part(0)
            v_part(0)
            u_part(1)
            v_part(1)
            sl, slr = next_slab("in0_p")
            for blk in range(NB):
                pt, pr = next_bank()
                cx.op("pe", mm_group(pt[:], [(hn[:, k, blk * 128:(blk + 1) * 128], sl[:, k, :]) for k in range(8)]),
                      reads=[slr, hnr[blk]], writes=[pr])
                cx.op("act", lambda e: e.copy(out=ptm_ap[blk], in_=pt[:]), reads=[pr], writes=[ptr[blk]])
            def sgu(h, s):
                pt, pr = next_bank()

                def fn(e):
                    ins = None
                    for b4 in range(4):
                        ins = e.matmul(pt[:, b4 * 128:(b4 + 1) * 128], lhsT=vn_ap[4 * s + b4][:, h * 128:(h + 1) * 128],
                                       rhs=wtb[:, h, :], start=True, stop=True)
                    return ins
                cx.op("pe", fn, reads=[wtb_r] + [vnr[4 * s + i] for i in range(4)], writes=[pr])
                ft, ftr = frot.get()
                cx.op("dve", lambda e: e.scalar_tensor_tensor(
                    out=ft[:, 0:512].rearrange("p (r n) -> p r n", r=4), in0=pt[:].rearrange("p (r n) -> p r n", r=4),
                    scalar=vec(V_LNG + h), in1=Ct[:, h, :].unsqueeze(1).to_broadcast([128, 4, 128]),
                    op0=ALU.mult, op1=ALU.add),
                    reads=[pr, Ct_r, cstf_r], writes=[ftr])
                ms = mixin[:, h, s * 512:(s + 1) * 512]
                cx.op("dve", lambda e: e.tensor_tensor(out=ms, in0=ft[:, 0:512], in1=ms, op=ALU.mult),
                      reads=[ftr, mx[h][s]], writes=[mx[h][s]])

            def band(g, s):
                pt, pr = next_bank()
                reads = [cstb_r]

                def fn(e):
                    ins = None
                    for b4 in range(4):
                        blk = 4 * s + b4
                        first = first_of_seq and blk == 0
                        o = pt[:, b4 * 128:(b4 + 1) * 128]
                        ins = e.matmul(o, lhsT=ptm_ap[blk][:, g * 128:(g + 1) * 128], rhs=Bmat(g, 2 if first else 0),
                                       start=True, stop=first)
                        if not first:
                            prev = ptm_ap[blk - 1] if blk > 0 else p_halo[:]
                            ins = e.matmul(o, lhsT=prev[:, g * 128:(g + 1) * 128], rhs=Bmat(g, 1), start=False, stop=True)
                    return ins
                for b4 in range(4):
                    blk = 4 * s + b4
                    reads.append(ptr[blk])
                    reads.append(ptr[blk - 1] if blk > 0 else p_halo_r)
                cx.op("pe", fn, reads=reads, writes=[pr])
                da, dr = brot.get()
                cx.op("act", lambda e: e.copy(out=da, in_=pt[:]), reads=[pr], writes=[dr])
                return da, dr

            def poolw(g, s, da, dr):
                p2, p2r = next_bank()
                cx.op("pe", lambda e: e.matmul(p2[:], lhsT=pwb[:, g, :], rhs=da, start=True, stop=True),
                      reads=[dr, pwb_r], writes=[p2r])
                cx.op("dve", lambda e: e.tensor_scalar(out=mixin[:, 4 + g, s * 512:(s + 1) * 512], in0=p2[:],
                                                       scalar1=vec(V_PB + g), scalar2=vec(V_PS + g),
                                                       op0=ALU.add, op1=ALU.mult),
                      reads=[p2r, cstf_r], writes=[mx[4 + g][s]])

            pend = None
            for n in range(8):
                g, s = divmod(n, 2)
                da, dr = band(g, s)
                sgu(g, s)
                if pend is not None:
                    poolw(*pend)
                pend = (g, s, da, dr)
            poolw(*pend)
            cx.op("act", lambda e: e.copy(out=p_halo[:], in_=ptm_ap[NB - 1]), reads=[ptr[NB - 1]], writes=[p_halo_r])
            residual_phase("out0_", mixin, lambda blk: [mx[k][blk // 4] for k in range(8)], norm_stages(V_N_FFN0))
            ffn_phase(0, tail)

        def layer1(first_of_seq, tail, pre_tail=None, prefetch=None):
            def conv_groups(c, sl, slr, s):
                rhs = [hn[:, k, s * 512:(s + 1) * 512] for k in range(8)]
                hreads = [hnr[4 * s + i] for i in range(4)]
                banks = []
                for sec in (2, 1, 0):
                    pt, pr = next_bank()
                    cx.op("pe", mm_group(pt[:], [(sl[:, k, sec, :], rhs[k]) for k in range(8)]),
                          reads=[slr] + hreads, writes=[pr])
                    banks.append((pt, pr))
                (ph, phr), (pc, pcr), (pbk, pbr) = banks
                hv, hvr = frot.get()
                cx.op("act", lambda e: e.copy(out=hv[:, 0:512], in_=ph[:]), reads=[phr], writes=[hvr])
                q, qr = frot.get()
                if s == 0 and first_of_seq:
                    cx.op("dve", lambda e: e.memset(q[:, 0:2], 0.0), writes=[qr])
                else:
                    cx.op("dve", lambda e: e.tensor_copy(out=q[:, 0:2], in_=qtail[:, c, :]),
                          reads=[qtail_r[c]], writes=[qr])
                cx.op("dve", lambda e: e.tensor_tensor(out=q[:, 2:514], in0=pc[:], in1=hv[:, 0:512], op=ALU.mult),
                      reads=[pcr, hvr], writes=[qr])
                cx.op("dve", lambda e: e.tensor_copy(out=qtail[:, c, :], in_=q[:, 512:514]),
                      reads=[qr], writes=[qtail_r[c]])
                t1, t1r = frot.get()
                cx.op("act", lambda e: e.activation(out=t1[:, 0:512], in_=q[:, 2:514], func=AF.Identity,
                                                    scale=vec(V_CW + 2 * 8 + c), bias=vec(V_CB + c)),
                      reads=[qr, cstf_r], writes=[t1r])
                cx.op("dve", lambda e: e.scalar_tensor_tensor(out=t1[:, 0:512], in0=q[:, 1:513], scalar=vec(V_CW + 8 + c),
                                                              in1=t1[:, 0:512], op0=ALU.mult, op1=ALU.add),
                      reads=[qr, t1r, cstf_r], writes=[t1r])
                cx.op("dve", lambda e: e.scalar_tensor_tensor(out=t1[:, 0:512], in0=q[:, 0:512], scalar=vec(V_CW + c),
                                                              in1=t1[:, 0:512], op0=ALU.mult, op1=ALU.add),
                      reads=[qr, t1r, cstf_r], writes=[t1r])
                cx.op("dve", lambda e: e.tensor_tensor(out=mixin[:, c, s * 512:(s + 1) * 512], in0=pbk[:],
                                                       in1=t1[:, 0:512], op=ALU.mult),
                      reads=[pbr, t1r], writes=[mx[c][s]])
            c0 = next_slab("in1_0")
            c1 = next_slab("in1_1", hold=1)
            conv_groups(0, c0[0], c0[1], 0)
            conv_groups(1, c1[0], c1[1], 0)
            conv_groups(0, c0[0], c0[1], 1)
            conv_groups(1, c1[0], c1[1], 1)
            for c in range(2, 8):
                sl, slr = next_slab("in1_%d" % c)
                for s in range(2):
                    conv_groups(c, sl, slr, s)
            residual_phase("out1_", mixin, lambda blk: [mx[k][blk // 4] for k in range(8)], norm_stages(V_N_FFN1))
            if prefetch:
                prefetch()
            ffn_phase(1, tail, pre_tail)

        def load_x(pi, blk):
            t0 = pi * T + blk * 128
            cx.dma("sp", x_tm[:, blk, :], x_d[t0:t0 + 128, :], xr[blk])

        first_gcol = V_N_EVEN if layers[0] == 0 else V_N_ODD
        for blk in range(NB):
            load_x(0, blk)
        pipeline(NB, None, norm_stages(first_gcol))
        ob = {"i": 0}

        def final_stages(pi):
            stt = {}

            def F2(i):
                t0 = pi * T + i * 128
                if final:
                    sl, sreg, col = stt[i]
                    obt, obr = outb_t[ob["i"]], outb_r[ob["i"]]
                    ob["i"] = (ob["i"] + 1) % 2
                    cx.op("dve", lambda e: e.scalar_tensor_tensor(out=obt[:], in0=x_tm[:, i, :], scalar=sl[:, col:col + 1],
                                                                  in1=cstf[:, C_FING:C_FING + D],
                                                                  op0=ALU.mult, op1=ALU.mult),
                          reads=[xr[i], sreg, cstf_r], writes=[obr])
                    cx.dma("sp", y_d[t0:t0 + 128, :], obt[:], obr, store=True)
                else:
                    cx.dma("sp", y_d[t0:t0 + 128, :], x_tm[:, i, :], xr[i], store=True)
                if pi + 1 < npass:
                    load_x(pi + 1, i)
            st = (stats_stages(0, stt, lagR=1) if final else []) + [(5, F2, lambda i: [xr[i]])]
            if pi + 1 < npass:
                st += norm_stages(first_gcol, lag0=7)
            return st

        def staged_boundary(pi):
            nxt = pi + 1 < npass
            fstat = {}
            nstat = {}

            def prefetch():
                for b in range(2):
                    t0 = (pi + 1) * T + b * 512
                    dst = stg1 if b == 0 else stg2[:]
                    cx.dma("sp", dst, x_d[t0:t0 + 512, :].rearrange("(j p) d -> p j d", p=128), stg_r2[b],
                           writes=stg_alias_all[b])

            def F2(i):
                t0 = pi * T + i * 128
                sl, sreg, col = fstat[i]
                obt, obr = outb_t[ob["i"]], outb_r[ob["i"]]
                ob["i"] = (ob["i"] + 1) % 2
                cx.op("dve", lambda e: e.scalar_tensor_tensor(out=obt[:], in0=x_tm[:, i, :], scalar=sl[:, col:col + 1],
                                                              in1=cstf[:, C_FING:C_FING + D],
                                                              op0=ALU.mult, op1=ALU.mult),
                      reads=[xr[i], sreg, cstf_r], writes=[obr])
                cx.dma("sp", y_d[t0:t0 + 128, :], obt[:], obr, store=True)
                if nxt:
                    load_x(pi + 1, i)
            tail = stats_stages(0, fstat, lagR=2)
            pre_tail = None
            if nxt:
                pre_tail = stats_stages(0, nstat, SRC_STG)
                tail = tail + norm_stages(first_gcol, 0, SRC_STG, stats=nstat, lagA=0, lagB=1)
            tail = tail + [(9, F2, lambda i: [xr[i]])]
            return (prefetch if nxt else None), pre_tail, tail

        for pi in range(npass):
            first_of_seq = (pi % PASS_PER_SEQ == 0)
            if 0 in layers:
                layer0(first_of_seq, norm_stages(V_N_ODD) if 1 in layers else final_stages(pi))
            if 1 in layers:
                if final and STAGED:
                    pf, pre_tail, tail = staged_boundary(pi)
                    layer1(first_of_seq, tail, pre_tail, pf)
                else:
                    layer1(first_of_seq, final_stages(pi))
        cx.flush_all()
        for r in (outb_r if final else xr):
            if r.dcnt:
                nc.sync.wait_ge(r.dsem, r.dcnt)
    return nc


def _pool_mats():
    B = np.zeros((128, 4, 3, 128), np.float32)
    for g, w in enumerate(POOL_WINDOWS):
        for i in range(128):
            for j in range(max(0, i - w + 1), i + 1):
                B[j, g, 0, i] += 1.0 / w
            B[i, g, 0, i] -= 1.0
            for jj in range(i - w + 1, 0):
                B[128 + jj, g, 1, i] += 1.0 / w
            cnt = min(i + 1, w)
            for j in range(max(0, i - w + 1), i + 1):
                B[j, g, 2, i] += 1.0 / cnt
            B[i, g, 2, i] -= 1.0
    return B


def _prep_consts(inp):
    f = lambda a: np.asarray(a, np.float32)
    cstf = np.zeros((128, NCF), np.float32)
    bs = f(inp["even_sgu_bs"])[0]
    cstf[:, C_BSB:C_BSB + 512] = np.broadcast_to(bs.T.reshape(1, 512), (128, 512))
    cstf[:, C_FING:C_FING + D] = np.broadcast_to(f(inp["final_norm"]).reshape(1, D), (128, D))
    vec = cstf[:, C_VEC:]
    vec[:, V_LNG:V_LNG + 4] = f(inp["even_sgu_ln_g"])[0].reshape(4, 128).T
    vec[:, V_LNB:V_LNB + 4] = f(inp["even_sgu_ln_b"])[0].reshape(4, 128).T
    vec[:, V_PB:V_PB + 4] = f(inp["even_pool_b"])[0].T
    vec[:, V_PS:V_PS + 4] = f(inp["even_pool_scale"])[0].reshape(4, 128).T
    cw = f(inp["odd_conv_w"])[0]
    vec[:, V_CW:V_CW + 24] = cw.reshape(3, 8, 128).transpose(2, 0, 1).reshape(128, 24)
    vec[:, V_CB:V_CB + 8] = f(inp["odd_conv_b"])[0].reshape(8, 128).T
    vec[:, V_N_EVEN:V_N_EVEN + 8] = f(inp["even_norm"])[0].reshape(8, 128).T
    vec[:, V_N_ODD:V_N_ODD + 8] = f(inp["odd_norm"])[0].reshape(8, 128).T
    vec[:, V_N_FFN0:V_N_FFN0 + 8] = f(inp["ffn_norm"])[0].reshape(8, 128).T
    vec[:, V_N_FFN1:V_N_FFN1 + 8] = f(inp["ffn_norm"])[1].reshape(8, 128).T
    vec[:, V_EPS] = EPS
    vec[:, V_MHALF] = -0.5
    vec[:, V_1P5] = 1.5
    vec[:, V_MAGIC] = 1597463007.0
    cstb = np.zeros((128, NCB), np.float32)
    cstb[:, CB_ID:CB_ID + 128] = np.eye(128, dtype=np.float32)
    cstb[:, CB_B:] = _pool_mats().reshape(128, 4 * 3 * 128)
    wst = np.ascontiguousarray(f(inp["even_sgu_ws"])[0].transpose(2, 0, 1))
    pw = np.ascontiguousarray(f(inp["even_pool_w"])[0].transpose(1, 0, 2))
    c = lambda a: np.ascontiguousarray(f(a))
    return {
        "cstf": cstf, "cstb": cstb, "wst": wst, "pw": pw,
        "win0": c(inp["even_w_in"][0]), "wout0": c(inp["even_w_out"][0]),
        "win1": c(inp["odd_w_in"][0]), "wout1": c(inp["odd_w_out"][0]),
        "wg0": c(inp["ffn_w_gate"][0]), "wg1": c(inp["ffn_w_gate"][1]),
        "wu0": c(inp["ffn_w_up"][0]), "wu1": c(inp["ffn_w_up"][1]),
        "wd0": c(inp["ffn_w_down"][0]), "wd1": c(inp["ffn_w_down"][1]),
    }


_NC_CACHE = {}


def _get_nc(layers, final):
    key = (tuple(layers), final)
    if key not in _NC_CACHE:
        _NC_CACHE[key] = build_nc(layers=layers, final=final)
    return _NC_CACHE[key]


def _launch(xs, consts, layers, final):
    nc = _get_nc(layers, final)
    in_maps = [dict(consts, x=xs[c]) for c in range(NCORES)]
    res = run_bass_kernel_spmd(nc, in_maps, core_ids=list(range(NCORES)))
    return [np.asarray(res.results[c]["y"]) for c in range(NCORES)]


def kernel(**inputs):
    x = np.asarray(inputs["x"], np.float32)
    consts = _prep_consts(inputs)
    xs = [np.ascontiguousarray(x[2 * c:2 * c + 2].reshape(TOK_CORE, D)) for c in range(NCORES)]
    ys = _launch(xs, consts, (0, 1), True)
    out = np.stack([y.reshape(2, SEQ, D) for y in ys], axis=0).reshape(16, SEQ, D)
    return out.astype(np.float32)
```

```python
import contextlib
import numpy as np
import concourse.bass as bass
import concourse.mybir as mybir
from concourse.bass_utils import run_bass_kernel_spmd

F32 = mybir.dt.float32
BF16 = mybir.dt.bfloat16
I32 = mybir.dt.int32
AF = mybir.ActivationFunctionType
ALU = mybir.AluOpType

D = 1024
DFF = 2816
NF = 22
T = 1024
NB = 8
NCORES = 8
TOK_CORE = 8192
SEQ = 4096
PASS_PER_SEQ = SEQ // T
EPS = 1e-6
POOL_WINDOWS = (2, 4, 8, 16)
RING_SLOTS = 4
PUMP_EVERY = 2
RSQRT_ENG = "dve"
STAGED = True
DOWN_F = ((0, 6), (6, 14), (14, 22))
STRICT_SAME_ENGINE = False
RING_ELEMS = 4096

C_BSB = 0
C_FING = C_BSB + 512
C_VEC = C_FING + 1024
V_LNG, V_LNB, V_PB, V_PS = 0, 4, 8, 12
V_CW = 16
V_CB = 40
V_N_EVEN, V_N_ODD, V_N_FFN0, V_N_FFN1 = 48, 56, 64, 72
V_EPS, V_MHALF, V_1P5, V_MAGIC = 80, 81, 82, 83
NVEC = 84
NCF = C_VEC + NVEC
CB_ID = 0
CB_B = 128
NCB = CB_B + 4 * 3 * 128


class Region:
    __slots__ = ("name", "w", "r", "dsem", "dkey", "dcnt", "pend")

    def __init__(self, name):
        self.name = name
        self.w = None
        self.r = {}
        self.dsem = None
        self.dkey = None
        self.dcnt = 0
        self.pend = 0


class Ctx:
    def __init__(self, nc, stack):
        self.nc = nc
        self.stack = stack
        self.eng = {"pe": nc.tensor, "act": nc.scalar, "dve": nc.vector, "pool": nc.gpsimd, "sp": nc.sync}
        self.semh = {}
        self.cnt = {}
        self.seen = {e: {} for e in self.eng}
        self.deferred = []
        self.in_flush = False
        self.pe_ops = 0
        for e in self.eng:
            self.semh[e] = stack.enter_context(nc.semaphore("sem_" + e))
            self.cnt[e] = 0

    def dma_region(self, name):
        r = Region(name)
        r.dkey = "dma_" + name
        r.dsem = self.stack.enter_context(self.nc.semaphore("sd_" + name))
        self.semh[r.dkey] = r.dsem
        return r

    def defer_tick(self, items):
        if not self.deferred:
            self.pe_ops = 0
        for _, touches in items:
            for r in touches:
                r.pend += 1
        self.deferred.append(items)

    def pump_tick(self):
        if not self.deferred:
            return
        items = self.deferred.pop(0)
        self.in_flush = True
        for fn, touches in items:
            for r in touches:
                r.pend -= 1
            fn()
        self.in_flush = False

    def flush_all(self):
        while self.deferred:
            self.pump_tick()

    def _sync_deferred(self, regs):
        if self.in_flush:
            return
        while self.deferred and any(r.pend for r in regs):
            self.pump_tick()

    def _waits(self, me, reads, writes):
        need = {}

        def add(h, raw):
            if h is None:
                return
            key, val = h
            if key == me and not raw and (me == "pe" or not STRICT_SAME_ENGINE):
                return
            if val > need.get(key, 0):
                need[key] = val
        for r in reads:
            add(r.w, True)
        for w in writes:
            add(w.w, False)
            for key, val in w.r.items():
                add((key, val), False)
        seen = self.seen[me]
        for key, val in need.items():
            if seen.get(key, 0) >= val:
                continue
            self.eng[me].wait_ge(self.semh[key], val)
            seen[key] = val

    def op(self, me, fn, reads=(), writes=()):
        self._sync_deferred(list(reads) + list(writes))
        self._waits(me, reads, writes)
        ins = fn(self.eng[me])
        self.cnt[me] += 1
        ins.then_inc(self.semh[me], 1)
        h = (me, self.cnt[me])
        for r in reads:
            if h[1] > r.r.get(me, 0):
                r.r[me] = h[1]
        for w in writes:
            w.w = h
            w.r = {}
        if me == "pe" and not self.in_flush and self.deferred:
            self.pe_ops += 1
            if self.pe_ops % PUMP_EVERY == 0:
                self.pump_tick()
        return h

    def dma(self, q, out_ap, in_ap, owner, store=False, reads=(), writes=()):
        if store:
            reads = list(reads) + [owner]
        else:
            writes = list(writes) + [owner]
        self._sync_deferred(list(reads) + list(writes))
        self._waits(q, reads, writes)
        outs = out_ap if isinstance(out_ap, (list, tuple)) else [out_ap]
        ins_ = in_ap if isinstance(in_ap, (list, tuple)) else [in_ap]
        for o, i in zip(outs, ins_):
            ins = self.eng[q].dma_start(out=o, in_=i)
            owner.dcnt += 16
            ins.then_inc(owner.dsem, 16)
        h = (owner.dkey, owner.dcnt)
        for r in reads:
            if h[1] > r.r.get(h[0], 0):
                r.r[h[0]] = h[1]
        for w in writes:
            w.w = h
            w.r = {}
        return h


class Rot:
    def __init__(self, aps, name):
        self.aps = aps
        self.regs = [Region("%s%d" % (name, i)) for i in range(len(aps))]
        self.i = 0

    def get(self):
        i = self.i
        self.i = (i + 1) % len(self.aps)
        return self.aps[i], self.regs[i]


def build_nc(layers=(0, 1), final=True, npass=TOK_CORE // T, dbg=""):
    nc = bass.Bass("TRN2", target_bir_lowering=False)
    dt_in = lambda name, shape: nc.dram_tensor(name, list(shape), F32, kind="ExternalInput").ap()
    x_d = dt_in("x", (TOK_CORE, D))
    cstf_d = dt_in("cstf", (128, NCF))
    cstb_d = dt_in("cstb", (128, NCB))
    wst_d = dt_in("wst", (128, 4, 128))
    pw_d = dt_in("pw", (128, 4, 128))
    win0_d = dt_in("win0", (D, 1536))
    wout0_d = dt_in("wout0", (D, D))
    win1_d = dt_in("win1", (D, 3 * D))
    wout1_d = dt_in("wout1", (D, D))
    wg_d = [dt_in("wg%d" % l, (D, DFF)) for l in range(2)]
    wu_d = [dt_in("wu%d" % l, (D, DFF)) for l in range(2)]
    wd_d = [dt_in("wd%d" % l, (DFF, D)) for l in range(2)]
    y_d = nc.dram_tensor("y", [TOK_CORE, D], F32, kind="ExternalOutput").ap()

    with contextlib.ExitStack() as stack:
        cx = Ctx(nc, stack)
        sb = lambda name, shape, dt: stack.enter_context(nc.sbuf_tensor(name, list(shape), dt))
        x_tm = sb("x_tm", (128, NB, D), F32)
        xr = [cx.dma_region("x%d" % b) for b in range(NB)]
        junk = sb("junk", (128, D), BF16)
        junk_r = Region("junk")
        xn_t = [sb("xn%d" % i, (128, D), BF16) for i in range(4)]
        xn_rot = Rot([t[:] for t in xn_t], "xn")
        st_t = [sb("st%d" % i, (128, 16), F32) for i in range(10)]
        st_rot = Rot(st_t, "st")
        hn = sb("hn", (128, 8, T), BF16)
        hnr = [Region("hn%d" % b) for b in range(NB)]
        mixin = sb("mixin", (128, 8, T), BF16)
        mx = [[Region("mx%d_%d" % (k, s)) for s in range(2)] for k in range(8)]
        hbuf = sb("hbuf", (128, NF, T), BF16)
        hr = [[Region("h%d_%d" % (f, s)) for s in range(2)] for f in range(NF)]
        vn_ap = [hbuf[:, b, 0:512] for b in range(NB)]
        vnr = [hr[b][0] for b in range(NB)]
        ptm_ap = [hbuf[:, 8 + b, 0:512] for b in range(NB)]
        ptr = [hr[8 + b][0] for b in range(NB)]
        stg2 = sb("stg2", (128, 4, D), F32)
        stg1 = mixin[:].rearrange("p k t -> p (k t)").bitcast(F32).rearrange("p (j d) -> p j d", j=4)
        stg_ap = [stg1[:, j, :] for j in range(4)] + [stg2[:, j, :] for j in range(4)]
        stg_r2 = [cx.dma_region("stg%d" % j) for j in range(2)]
        stg_r = [stg_r2[j // 4] for j in range(8)]
        stg_alias_all = [[mx[k][s_] for k in range(8) for s_ in range(2)], []]
        stg_alias = [[mx[2 * j][0], mx[2 * j][1], mx[2 * j + 1][0], mx[2 * j + 1][1]] for j in range(4)] + [[] for _ in range(4)]
        p_halo = sb("p_halo", (128, 512), BF16)
        p_halo_r = Region("p_halo")
        fr_t = [sb("fr%d" % i, (128, 514), F32) for i in range(6)]
        frot = Rot(fr_t, "fr")
        br_t = [sb("br%d" % i, (128, 512), BF16) for i in range(3)]
        brot = Rot([t[:] for t in br_t], "br")
        bn_t = [sb("bn%d" % i, (128, 6), F32) for i in range(4)]
        bn_rot = Rot([t[:] for t in bn_t], "bn")
        mv_t = [sb("mv%d" % i, (128, 8), F32) for i in range(2)]
        mv_rot = Rot([t[:] for t in mv_t], "mv")
        outb_t = [sb("outb%d" % i, (128, D), F32) for i in range(2)]
        outb_r = [cx.dma_region("outb%d" % i) for i in range(2)]
        qtail = sb("qtail", (128, 8, 2), F32)
        qtail_r = [Region("qt%d" % c) for c in range(8)]
        ring_t = [sb("ring%d" % i, (128, RING_ELEMS), BF16) for i in range(RING_SLOTS)]
        ring_r = [cx.dma_region("ring%d" % i) for i in range(RING_SLOTS)]
        cstf = sb("cstf_s", (128, NCF), F32)
        cstf_r = cx.dma_region("cstf")
        cstb = sb("cstb_s", (128, NCB), BF16)
        cstb_r = cx.dma_region("cstb")
        wtb = sb("wtb", (128, 4, 128), BF16)
        wtb_r = cstb_r
        pwb = sb("pwb", (128, 4, 128), BF16)
        pwb_r = cstb_r
        ones_b = sb("ones_b", (128, 128), BF16)
        ones_r = Region("ones")
        Ct = sb("Ct", (128, 4, 128), F32)
        pbs = sb("pbs", (128, 4), F32)
        pbs_r = Region("pbs")
        Ct_r = Region("Ct")
        ps_t = [stack.enter_context(nc.psum_tensor("ps%d" % i, [128, 512], F32)) for i in range(8)]
        ps_r = [Region("ps%d" % i) for i in range(8)]
        bank_i = [0]

        def next_bank():
            i = bank_i[0]
            bank_i[0] = (i + 1) % 8
            return ps_t[i], ps_r[i]
        bank_override = [None]

        def pick_bank(i):
            return bank_override[0](i) if bank_override[0] else next_bank()

        vec = lambda col, n=1: cstf[:, C_VEC + col:C_VEC + col + n]
        ident_b = cstb[:, CB_ID:CB_ID + 128]

        def Bmat(g, kind):
            o = CB_B + (g * 3 + kind) * 128
            return cstb[:, o:o + 128]

        kview = lambda w: w.rearrange("(k p) n -> p k n", p=128)
        slabs = []

        def slab_list():
            lst = []
            for l in layers:
                if l == 0:
                    v = kview(win0_d)
                    for j, nm in enumerate(("in0_u", "in0_v", "in0_p")):
                        lst.append((nm, v[:, :, j * 512:(j + 1) * 512], "k512"))
                    v = kview(wout0_d)
                    for j in range(2):
                        lst.append(("out0_%d" % j, v[:, :, j * 512:(j + 1) * 512], "k512"))
                else:
                    v = win1_d.rearrange("(k p) (s c n) -> p k s c n", p=128, s=3, c=8)
                    for c in range(8):
                        lst.append(("in1_%d" % c, [v[:, :, sec, c, :] for sec in range(3)], "k3x128"))
                    v = kview(wout1_d)
                    for j in range(2):
                        lst.append(("out1_%d" % j, v[:, :, j * 512:(j + 1) * 512], "k512"))
                vg, vu = kview(wg_d[l]), kview(wu_d[l])
                for c in range(11):
                    lst.append(("gu%d_%d" % (l, c), [vg[:, :, c * 256:(c + 1) * 256], vu[:, :, c * 256:(c + 1) * 256]], "gu"))
                vd = wd_d[l].rearrange("(f p) n -> p f n", p=128)
                for half in range(2):
                    for fg in range(3):
                        f0, f1 = DOWN_F[fg]
                        lst.append(("d%d_%d_%d" % (l, half, fg),
                                    vd[:, f0:f1, half * 512:(half + 1) * 512], "f%d" % (f1 - f0)))
            return lst

        per_pass = slab_list()
        all_slabs = per_pass * npass
        st_ = {"issued": 0, "next": 0}

        def slab_view(slot, kind):
            t = ring_t[slot]
            if kind in ("k512", "gu"):
                return t[:, 0:4096].rearrange("p (k n) -> p k n", k=8)
            if kind == "k256":
                return t[:, 0:2048].rearrange("p (k n) -> p k n", k=8)
            if kind == "k3x128":
                return t[:, 0:3072].rearrange("p (k s n) -> p k s n", k=8, s=3)
            if kind == "f8":
                return t[:, 0:4096].rearrange("p (f n) -> p f n", f=8)
            if kind == "f6":
                return t[:, 0:3072].rearrange("p (f n) -> p f n", f=6)
            raise ValueError(kind)

        def issue_upto(j):
            while st_["issued"] <= j and st_["issued"] < len(all_slabs):
                i = st_["issued"]
                nm, src, kind = all_slabs[i]
                slot = i % RING_SLOTS
                v = slab_view(slot, kind)
                if kind == "gu":
                    cx.dma("pool", [v[:, :, 0:256], v[:, :, 256:512]], src, ring_r[slot])
                elif kind == "k3x128":
                    cx.dma("pool", [v[:, :, sec, :] for sec in range(3)], src, ring_r[slot])
                else:
                    cx.dma("pool", v, src, ring_r[slot])
                st_["issued"] += 1

        def next_slab(prefix, hold=0):
            j = st_["next"]
            st_["next"] += 1
            nm, src, kind = all_slabs[j]
            assert nm.startswith(prefix), (nm, prefix)
            issue_upto(j - hold + RING_SLOTS - 1)
            slot = j % RING_SLOTS
            return slab_view(slot, kind), ring_r[slot]

        cx.dma("sp", cstf[:], cstf_d, cstf_r)
        cx.dma("pool", [cstb[:], wtb[:], pwb[:]], [cstb_d, wst_d, pw_d], cstb_r)
        cx.op("dve", lambda e: e.memset(ones_b[:], 1.0), writes=[ones_r])
        cx.op("dve", lambda e: e.memset(wtb[64:128, :, 0:64], 0.0), writes=[wtb_r])
        cx.op("dve", lambda e: e.memset(qtail[:], 0.0), writes=qtail_r)
        cx.op("dve", lambda e: e.memset(p_halo[:], 0.0), writes=[p_halo_r])
        cx.op("dve", lambda e: e.tensor_tensor(out=pbs[:], in0=vec(V_PB, 4), in1=vec(V_PS, 4), op=ALU.mult),
              reads=[cstf_r], writes=[pbs_r])
        if 0 in layers:
            for h in range(4):
                pt, pr = next_bank()
                cx.op("pe", lambda e: e.matmul(pt[:, 0:128], lhsT=ones_b[:], rhs=wtb[:, h, :], start=True, stop=True),
                      reads=[ones_r, wtb_r], writes=[pr])
                cx.op("dve", lambda e: e.scalar_tensor_tensor(
                    out=Ct[:, h, :], in0=pt[:, 0:128], scalar=vec(V_LNB + h),
                    in1=cstf[:, C_BSB + h * 128:C_BSB + (h + 1) * 128], op0=ALU.mult, op1=ALU.add),
                    reads=[pr, cstf_r], writes=[Ct_r])

        def rsqrt_batch(stt, str_, n, iters=2):
            a = stt[:, 0:n]
            y = stt[:, 4:4 + n]
            t = stt[:, 8:8 + n]
            t0 = stt[:, 12:12 + n]
            D_ = lambda fn: cx.op("dve", fn, reads=[str_], writes=[str_])
            D_(lambda e: e.tensor_scalar(out=a, in0=a, scalar1=EPS, scalar2=None, op0=ALU.add))
            D_(lambda e: e.tensor_copy(out=t0, in_=a.bitcast(I32)))
            D_(lambda e: e.tensor_scalar(out=y.bitcast(I32), in0=t0, scalar1=-0.5, scalar2=1597463007.0,
                                         op0=ALU.mult, op1=ALU.add))
            for _ in range(iters):
                D_(lambda e: e.tensor_tensor(out=t, in0=a, in1=y, op=ALU.mult))
                D_(lambda e: e.tensor_tensor(out=t, in0=t, in1=y, op=ALU.mult))
                D_(lambda e: e.tensor_scalar(out=t, in0=t, scalar1=-0.5, scalar2=1.5, op0=ALU.mult, op1=ALU.add))
                D_(lambda e: e.tensor_tensor(out=y, in0=y, in1=t, op=ALU.mult))

        def rsqrt_pool(sl, sreg):
            ms, t0, t, ya, yb = (sl[:, i:i + 1] for i in range(5))
            P_ = lambda fn: cx.op(RSQRT_ENG, fn, reads=[sreg, cstf_r], writes=[sreg])
            P_(lambda e: e.tensor_copy(out=t0, in_=ms.bitcast(I32)))
            P_(lambda e: e.tensor_scalar(out=ya.bitcast(I32), in0=t0, scalar1=vec(V_MHALF), scalar2=vec(V_MAGIC),
                                         op0=ALU.mult, op1=ALU.add))
            for (y0, y1) in ((ya, yb), (yb, ya)):
                P_(lambda e: e.tensor_scalar(out=t, in0=ms, scalar1=vec(V_EPS), scalar2=y0, op0=ALU.add, op1=ALU.mult))
                P_(lambda e: e.tensor_scalar(out=t, in0=t, scalar1=y0, scalar2=vec(V_MHALF), op0=ALU.mult, op1=ALU.mult))
                P_(lambda e: e.tensor_scalar(out=y1, in0=t, scalar1=vec(V_1P5), scalar2=y0, op0=ALU.add, op1=ALU.mult))

        SRC_X = (lambda i: x_tm[:, i, :], lambda i: [xr[i]])
        SRC_STG = (lambda i: stg_ap[i], lambda i: [stg_r[i]] + stg_alias[i])

        def stats_stages(lag0, store, src=None, lagR=0):
            cur = {}
            src_ap, src_regs = src or SRC_X

            def S1(i):
                if i % 4 == 0:
                    cur["t"] = st_rot.get()
                stt, str_ = cur["t"]
                store[i] = (stt, str_, 4 + i % 4)
                cx.op("act", lambda e: e.activation(out=junk[:], in_=src_ap(i), func=AF.Square,
                                                    scale=1.0 / 32.0, accum_out=stt[:, i % 4:i % 4 + 1]),
                      reads=src_regs(i), writes=[junk_r, str_])

            def R(i):
                if i % 4 == 3:
                    stt, str_, _ = store[i]
                    rsqrt_batch(stt, str_, 4)
            return [(lag0, S1, lambda i: src_regs(i)), (lag0 + lagR, R, lambda i: src_regs(i))]

        def norm_stages(gcol, lag0=0, src=None, stats=None, lagA=None, lagB=None):
            stt = stats if stats is not None else {}
            xbuf = {}
            src_ap, src_regs = src or SRC_X

            def A2(i):
                sl, sreg, col = stt[i]
                xa, xreg = xn_rot.get()
                xbuf[i] = (xa, xreg)
                cx.op("act", lambda e: e.mul(out=xa, in_=src_ap(i), mul=sl[:, col:col + 1]),
                      reads=src_regs(i) + [sreg], writes=[xreg])

            def B(i):
                xa, xreg = xbuf[i]
                pt, pr = pick_bank(i)
                pb = pt[:].bitcast(BF16)

                def tr(e):
                    ins = None
                    for k in range(8):
                        ins = e.transpose(out=pb[:, k * 128:(k + 1) * 128], in_=xa[:, k * 128:(k + 1) * 128],
                                          identity=ident_b)
                    return ins
                cx.op("pe", tr, reads=[xreg, cstb_r], writes=[pr])
                cx.op("dve", lambda e: e.tensor_tensor(
                    out=hn[:, :, i * 128:(i + 1) * 128],
                    in0=pb.rearrange("p (k n) -> p k n", k=8),
                    in1=vec(gcol, 8).unsqueeze(2).to_broadcast([128, 8, 128]), op=ALU.mult),
                    reads=[pr, cstf_r], writes=[hnr[i]])
            st = [] if stats is not None else stats_stages(lag0, stt, src)
            if lagA is None:
                tA = lambda i: lag0 + 4 * (i // 4) + 4 + (i % 4) // 2
                tB = lambda i: lag0 + 4 * (i // 4) + 5 + (i % 4) // 2
            else:
                tA, tB = lag0 + lagA, lag0 + lagB
            return st + [(tB, B, lambda i: src_regs(i) + [hnr[i]]), (tA, A2, lambda i: src_regs(i))]

        def pipeline(n, tick, stages):
            cx.flush_all()
            sched = {}
            for si, (lag, fn, touches) in enumerate(stages):
                for i in range(n):
                    t = lag(i) if callable(lag) else i + lag
                    sched.setdefault(t, []).append((si, i, fn, touches))
            tmax = max(list(sched) + [n - 1])
            for t in range(tmax + 1):
                items = sorted(sched.get(t, []), key=lambda it: (it[0], it[1]))
                if t < n:
                    if tick is not None:
                        tick(t)
                    for _, i, fn, _t in items:
                        fn(i)
                elif items:
                    cx.defer_tick([(lambda fn=fn, i=i: fn(i), touches(i)) for _, i, fn, touches in items])

        def mm_group(pt, pairs):
            def fn(e):
                ins = None
                n = len(pairs)
                for i, (l, r) in enumerate(pairs):
                    ins = e.matmul(pt, lhsT=l, rhs=r, start=(i == 0), stop=(i == n - 1))
                return ins
            return fn

        def residual_phase(prefix, src, src_regs_fn, tail):
            sl0 = next_slab(prefix)
            sl1 = next_slab(prefix, hold=1)

            def tick(blk):
                for half, (sl, slr) in enumerate((sl0, sl1)):
                    pt, pr = next_bank()
                    cx.op("pe", mm_group(pt[:], [(src[:, k, blk * 128:(blk + 1) * 128], sl[:, k, :]) for k in range(8)]),
                          reads=[slr] + src_regs_fn(blk), writes=[pr])
                    xs = x_tm[:, blk, half * 512:(half + 1) * 512]
                    cx.op("dve", lambda e: e.tensor_tensor(out=xs, in0=pt[:], in1=xs, op=ALU.add),
                          reads=[pr, xr[blk]], writes=[xr[blk]])
            pipeline(NB, tick, tail)

        def ffn_phase(l, tail, pre_tail=None):
            def gu_groups(c, sl, slr, s):
                for fi in range(2):
                    f = 2 * c + fi
                    rhs = [hn[:, k, s * 512:(s + 1) * 512] for k in range(8)]
                    hreads = [hnr[4 * s + i] for i in range(4)]
                    pa, par = next_bank()
                    cx.op("pe", mm_group(pa[:], [(sl[:, k, fi * 128:(fi + 1) * 128], rhs[k]) for k in range(8)]),
                          reads=[slr] + hreads, writes=[par])
                    pb_, pbr = next_bank()
                    cx.op("pe", mm_group(pb_[:], [(sl[:, k, 256 + fi * 128:256 + (fi + 1) * 128], rhs[k]) for k in range(8)]),
                          reads=[slr] + hreads, writes=[pbr])
                    ft, ftr = frot.get()
                    cx.op("act", lambda e: e.activation(out=ft[:, 0:512], in_=pa[:], func=AF.Silu),
                          reads=[par], writes=[ftr])
                    cx.op("dve", lambda e: e.tensor_tensor(out=hbuf[:, f, s * 512:(s + 1) * 512], in0=pb_[:],
                                                           in1=ft[:, 0:512], op=ALU.mult),
                          reads=[pbr, ftr], writes=[hr[f][s]])
            s0 = next_slab("gu%d_0" % l)
            s1 = next_slab("gu%d_1" % l, hold=1)
            gu_groups(0, s0[0], s0[1], 0)
            gu_groups(1, s1[0], s1[1], 0)
            gu_groups(0, s0[0], s0[1], 1)
            gu_groups(1, s1[0], s1[1], 1)
            for c in range(2, 11):
                sl, slr = next_slab("gu%d_%d" % (l, c))
                for s in range(2):
                    gu_groups(c, sl, slr, s)
            def down_group(blk, sl, slr, f0, f1):
                pt, pr = ps_t[blk], ps_r[blk]
                s = blk // 4

                def fn(e):
                    ins = None
                    for f in range(f0, f1):
                        ins = e.matmul(pt[:], lhsT=hbuf[:, f, blk * 128:(blk + 1) * 128], rhs=sl[:, f - f0, :],
                                       start=(f == 0), stop=(f == NF - 1))
                    return ins
                cx.op("pe", fn, reads=[slr] + [hr[f][s] for f in range(f0, f1)], writes=[pr])

            def down_add(blk, half):
                pt, pr = ps_t[blk], ps_r[blk]
                xs = x_tm[:, blk, half * 512:(half + 1) * 512]
                cx.op("dve", lambda e: e.tensor_tensor(out=xs, in0=pt[:], in1=xs, op=ALU.add),
                      reads=[pr, xr[blk]], writes=[xr[blk]])

            for fg in range(3):
                sl, slr = next_slab("d%d_0_%d" % (l, fg))
                for blk in range(NB):
                    down_group(blk, sl, slr, *DOWN_F[fg])
                    if fg == 2:
                        down_add(blk, 0)
            sl, slr = next_slab("d%d_1_0" % l)

            def tick_a(blk, sl=sl, slr=slr):
                down_group(blk, sl, slr, *DOWN_F[0])
            if pre_tail:
                pipeline(NB, tick_a, pre_tail)
            else:
                for blk in range(NB):
                    tick_a(blk)
            sa = next_slab("d%d_1_1" % l)
            sb_ = next_slab("d%d_1_2" % l, hold=1)

            def tick_b(blk):
                down_group(blk, sa[0], sa[1], *DOWN_F[1])
                down_group(blk, sb_[0], sb_[1], *DOWN_F[2])
                down_add(blk, 1)
            bank_override[0] = lambda i: (ps_t[i], ps_r[i])
            pipeline(NB, tick_b, tail)
            bank_override[0] = None
            bank_i[0] = 0

        def layer0(first_of_seq, tail):
            slu, slur = next_slab("in0_u")
            slv, slvr = next_slab("in0_v", hold=1)

            def u_part(s):
                for o in range(4):
                    pt, pr = next_bank()
                    cx.op("pe", mm_group(pt[:], [(slu[:, k, o * 128:(o + 1) * 128], hn[:, k, s * 512:(s + 1) * 512])
                                                 for k in range(8)]),
                          reads=[slur] + [hnr[4 * s + i] for i in range(4)], writes=[pr])
                    cx.op("act", lambda e: e.activation(out=mixin[:, o, s * 512:(s + 1) * 512], in_=pt[:], func=AF.Gelu),
                          reads=[pr], writes=[mx[o][s]])

            def v_part(s):
                stt, str_ = st_rot.get()
                mv, mvr = mv_rot.get()
                v32s = []
                for b4 in range(4):
                    blk = 4 * s + b4
                    pt, pr = next_bank()
                    cx.op("pe", mm_group(pt[:], [(hn[:, k, blk * 128:(blk + 1) * 128], slv[:, k, :]) for k in range(8)]),
                          reads=[slvr, hnr[blk]], writes=[pr])
                    ft, ftr = frot.get()
                    v32s.append((ft, ftr))
                    cx.op("act", lambda e: e.activation(out=ft[:, 0:512], in_=pt[:], func=AF.Gelu),
                          reads=[pr], writes=[ftr])
                    bn, bnr = bn_rot.get()
                    cx.op("dve", lambda e: e.bn_stats(out=bn, in_=ft[:, 0:512]), reads=[ftr], writes=[bnr])
                    cx.op("dve", lambda e: e.bn_aggr(out=mv[:, 2 * b4:2 * b4 + 2], in_=bn), reads=[bnr], writes=[mvr])
                cx.op("dve", lambda e: e.tensor_copy(out=stt[:, 0:4], in_=mv.rearrange("p (b t) -> p b t", t=2)[:, :, 1]),
                      reads=[mvr], writes=[str_])
                rsqrt_batch(stt, str_, 4, iters=3)
                for b4 in range(4):
                    blk = 4 * s + b4
                    ft, ftr = v32s[b4]
                    cx.op("dve", lambda e: e.tensor_scalar(out=vn_ap[blk], in0=ft[:, 0:512], scalar1=mv[:, 2 * b4:2 * b4 + 1],
                                                           scalar2=stt[:, 4 + b4:5 + b4], op0=ALU.subtract, op1=ALU.mult),
                          reads=[ftr, mvr, str_], writes=[vnr[blk]])
            u_part(0)
            v_part(0)
            u_part(1)
            v_part(1)
            sl, slr = next_slab("in0_p")
            for blk in range(NB):
                pt, pr = next_bank()
                cx.op("pe", mm_group(pt[:], [(hn[:, k, blk * 128:(blk + 1) * 128], sl[:, k, :]) for k in range(8)]),
                      reads=[slr, hnr[blk]], writes=[pr])
                cx.op("act", lambda e: e.copy(out=ptm_ap[blk], in_=pt[:]), reads=[pr], writes=[ptr[blk]])
            def sgu(h, s):
                pt, pr = next_bank()

                def fn(e):
                    ins = None
                    for b4 in range(4):
                        ins = e.matmul(pt[:, b4 * 128:(b4 + 1) * 128], lhsT=vn_ap[4 * s + b4][:, h * 128:(h + 1) * 128],
                                       rhs=wtb[:, h, :], start=True, stop=True)
                    return ins
                cx.op("pe", fn, reads=[wtb_r] + [vnr[4 * s + i] for i in range(4)], writes=[pr])
                ft, ftr = frot.get()
                cx.op("dve", lambda e: e.scalar_tensor_tensor(
                    out=ft[:, 0:512].rearrange("p (r n) -> p r n", r=4), in0=pt[:].rearrange("p (r n) -> p r n", r=4),
                    scalar=vec(V_LNG + h), in1=Ct[:, h, :].unsqueeze(1).to_broadcast([128, 4, 128]),
                    op0=ALU.mult, op1=ALU.add),
                    reads=[pr, Ct_r, cstf_r], writes=[ftr])
                ms = mixin[:, h, s * 512:(s + 1) * 512]
                cx.op("dve", lambda e: e.tensor_tensor(out=ms, in0=ft[:, 0:512], in1=ms, op=ALU.mult),
                      reads=[ftr, mx[h][s]], writes=[mx[h][s]])

            def band(g, s):
                pt, pr = next_bank()
                reads = [cstb_r]

                def fn(e):
                    ins = None
                    for b4 in range(4):
                        blk = 4 * s + b4
                        first = first_of_seq and blk == 0
                        o = pt[:, b4 * 128:(b4 + 1) * 128]
                        ins = e.matmul(o, lhsT=ptm_ap[blk][:, g * 128:(g + 1) * 128], rhs=Bmat(g, 2 if first else 0),
                                       start=True, stop=first)
                        if not first:
                            prev = ptm_ap[blk - 1] if blk > 0 else p_halo[:]
                            ins = e.matmul(o, lhsT=prev[:, g * 128:(g + 1) * 128], rhs=Bmat(g, 1), start=False, stop=True)
                    return ins
                for b4 in range(4):
                    blk = 4 * s + b4
                    reads.append(ptr[blk])
                    reads.append(ptr[blk - 1] if blk > 0 else p_halo_r)
                cx.op("pe", fn, reads=reads, writes=[pr])
                da, dr = brot.get()
                cx.op("act", lambda e: e.copy(out=da, in_=pt[:]), reads=[pr], writes=[dr])
                return da, dr

            def poolw(g, s, da, dr):
                p2, p2r = next_bank()
                cx.op("pe", lambda e: e.matmul(p2[:], lhsT=pwb[:, g, :], rhs=da, start=True, stop=True),
                      reads=[dr, pwb_r], writes=[p2r])
                cx.op("act", lambda e: e.activation(out=mixin[:, 4 + g, s * 512:(s + 1) * 512], in_=p2[:], func=AF.Identity,
                                                    scale=vec(V_PS + g), bias=pbs[:, g:g + 1]),
                      reads=[p2r, cstf_r, pbs_r], writes=[mx[4 + g][s]])

            pend = None
            for n in range(8):
                s, g = divmod(n, 4)
                da, dr = band(g, s)
                sgu(g, s)
                if pend is not None:
                    poolw(*pend)
                pend = (g, s, da, dr)
            poolw(*pend)
            cx.op("act", lambda e: e.copy(out=p_halo[:], in_=ptm_ap[NB - 1]), reads=[ptr[NB - 1]], writes=[p_halo_r])
            residual_phase("out0_", mixin, lambda blk: [mx[k][blk // 4] for k in range(8)], norm_stages(V_N_FFN0))
            ffn_phase(0, tail)

        def layer1(first_of_seq, tail, pre_tail=None, prefetch=None):
            def conv_groups(c, sl, slr, s):
                rhs = [hn[:, k, s * 512:(s + 1) * 512] for k in range(8)]
                hreads = [hnr[4 * s + i] for i in range(4)]
                banks = []
                for sec in (2, 1, 0):
                    pt, pr = next_bank()
                    cx.op("pe", mm_group(pt[:], [(sl[:, k, sec, :], rhs[k]) for k in range(8)]),
                          reads=[slr] + hreads, writes=[pr])
                    banks.append((pt, pr))
                (ph, phr), (pc, pcr), (pbk, pbr) = banks
                hv, hvr = frot.get()
                cx.op("act", lambda e: e.copy(out=hv[:, 0:512], in_=ph[:]), reads=[phr], writes=[hvr])
                q, qr = frot.get()
                if s == 0 and first_of_seq:
                    cx.op("dve", lambda e: e.memset(q[:, 0:2], 0.0), writes=[qr])
                else:
                    cx.op("dve", lambda e: e.tensor_copy(out=q[:, 0:2], in_=qtail[:, c, :]),
                          reads=[qtail_r[c]], writes=[qr])
                cx.op("dve", lambda e: e.tensor_tensor(out=q[:, 2:514], in0=pc[:], in1=hv[:, 0:512], op=ALU.mult),
                      reads=[pcr, hvr], writes=[qr])
                cx.op("dve", lambda e: e.tensor_copy(out=qtail[:, c, :], in_=q[:, 512:514]),
                      reads=[qr], writes=[qtail_r[c]])
                t1, t1r = frot.get()
                cx.op("act", lambda e: e.activation(out=t1[:, 0:512], in_=q[:, 2:514], func=AF.Identity,
                                                    scale=vec(V_CW + 2 * 8 + c), bias=vec(V_CB + c)),
                      reads=[qr, cstf_r], writes=[t1r])
                cx.op("dve", lambda e: e.scalar_tensor_tensor(out=t1[:, 0:512], in0=q[:, 1:513], scalar=vec(V_CW + 8 + c),
                                                              in1=t1[:, 0:512], op0=ALU.mult, op1=ALU.add),
                      reads=[qr, t1r, cstf_r], writes=[t1r])
                cx.op("dve", lambda e: e.scalar_tensor_tensor(out=t1[:, 0:512], in0=q[:, 0:512], scalar=vec(V_CW + c),
                                                              in1=t1[:, 0:512], op0=ALU.mult, op1=ALU.add),
                      reads=[qr, t1r, cstf_r], writes=[t1r])
                cx.op("dve", lambda e: e.tensor_tensor(out=mixin[:, c, s * 512:(s + 1) * 512], in0=pbk[:],
                                                       in1=t1[:, 0:512], op=ALU.mult),
                      reads=[pbr, t1r], writes=[mx[c][s]])
            c0 = next_slab("in1_0")
            c1 = next_slab("in1_1", hold=1)
            conv_groups(0, c0[0], c0[1], 0)
            conv_groups(1, c1[0], c1[1], 0)
            conv_groups(0, c0[0], c0[1], 1)
            conv_groups(1, c1[0], c1[1], 1)
            for c in range(2, 8):
                sl, slr = next_slab("in1_%d" % c)
                for s in range(2):
                    conv_groups(c, sl, slr, s)
            residual_phase("out1_", mixin, lambda blk: [mx[k][blk // 4] for k in range(8)], norm_stages(V_N_FFN1))
            if prefetch:
                prefetch()
            ffn_phase(1, tail, pre_tail)

        def load_x(pi, blk):
            t0 = pi * T + blk * 128
            cx.dma("sp", x_tm[:, blk, :], x_d[t0:t0 + 128, :], xr[blk])

        first_gcol = V_N_EVEN if layers[0] == 0 else V_N_ODD
        for blk in range(NB):
            load_x(0, blk)
        pipeline(NB, None, norm_stages(first_gcol))
        ob = {"i": 0}

        def final_stages(pi):
            stt = {}

            def F2(i):
                t0 = pi * T + i * 128
                if final:
                    sl, sreg, col = stt[i]
                    obt, obr = outb_t[ob["i"]], outb_r[ob["i"]]
                    ob["i"] = (ob["i"] + 1) % 2
                    cx.op("dve", lambda e: e.scalar_tensor_tensor(out=obt[:], in0=x_tm[:, i, :], scalar=sl[:, col:col + 1],
                                                                  in1=cstf[:, C_FING:C_FING + D],
                                                                  op0=ALU.mult, op1=ALU.mult),
                          reads=[xr[i], sreg, cstf_r], writes=[obr])
                    cx.dma("sp", y_d[t0:t0 + 128, :], obt[:], obr, store=True)
                else:
                    cx.dma("sp", y_d[t0:t0 + 128, :], x_tm[:, i, :], xr[i], store=True)
                if pi + 1 < npass:
                    load_x(pi + 1, i)
            st = (stats_stages(0, stt, lagR=1) if final else []) + [(5, F2, lambda i: [xr[i]])]
            if pi + 1 < npass:
                st += norm_stages(first_gcol, lag0=7)
            return st

        def staged_boundary(pi):
            nxt = pi + 1 < npass
            fstat = {}
            nstat = {}

            def prefetch():
                for b in range(2):
                    t0 = (pi + 1) * T + b * 512
                    dst = stg1 if b == 0 else stg2[:]
                    cx.dma("sp", dst, x_d[t0:t0 + 512, :].rearrange("(j p) d -> p j d", p=128), stg_r2[b],
                           writes=stg_alias_all[b])

            def F2(i):
                t0 = pi * T + i * 128
                sl, sreg, col = fstat[i]
                obt, obr = outb_t[ob["i"]], outb_r[ob["i"]]
                ob["i"] = (ob["i"] + 1) % 2
                cx.op("dve", lambda e: e.scalar_tensor_tensor(out=obt[:], in0=x_tm[:, i, :], scalar=sl[:, col:col + 1],
                                                              in1=cstf[:, C_FING:C_FING + D],
                                                              op0=ALU.mult, op1=ALU.mult),
                      reads=[xr[i], sreg, cstf_r], writes=[obr])
                cx.dma("sp", y_d[t0:t0 + 128, :], obt[:], obr, store=True)
                if nxt:
                    load_x(pi + 1, i)
            tail = stats_stages(0, fstat, lagR=2)
            pre_tail = None
            if nxt:
                pre_tail = stats_stages(0, nstat, SRC_STG)
                tail = tail + norm_stages(first_gcol, 0, SRC_STG, stats=nstat, lagA=0, lagB=1)
            tail = tail + [(9, F2, lambda i: [xr[i]])]
            return (prefetch if nxt else None), pre_tail, tail

        for pi in range(npass):
            first_of_seq = (pi % PASS_PER_SEQ == 0)
            if 0 in layers:
                layer0(first_of_seq, norm_stages(V_N_ODD) if 1 in layers else final_stages(pi))
            if 1 in layers:
                if final and STAGED:
                    pf, pre_tail, tail = staged_boundary(pi)
                    layer1(first_of_seq, tail, pre_tail, pf)
                else:
                    layer1(first_of_seq, final_stages(pi))
        cx.flush_all()
        for r in (outb_r if final else xr):
            if r.dcnt:
                nc.sync.wait_ge(r.dsem, r.dcnt)
    return nc


def _pool_mats():
    B = np.zeros((128, 4, 3, 128), np.float32)
    for g, w in enumerate(POOL_WINDOWS):
        for i in range(128):
            for j in range(max(0, i - w + 1), i + 1):
                B[j, g, 0, i] += 1.0 / w
            B[i, g, 0, i] -= 1.0
            for jj in range(i - w + 1, 0):
                B[128 + jj, g, 1, i] += 1.0 / w
            cnt = min(i + 1, w)
            for j in range(max(0, i - w + 1), i + 1):
                B[j, g, 2, i] += 1.0 / cnt
            B[i, g, 2, i] -= 1.0
    return B


def _prep_consts(inp):
    f = lambda a: np.asarray(a, np.float32)
    cstf = np.zeros((128, NCF), np.float32)
    bs = f(inp["even_sgu_bs"])[0]
    cstf[:, C_BSB:C_BSB + 512] = np.broadcast_to(bs.T.reshape(1, 512), (128, 512))
    cstf[:, C_FING:C_FING + D] = np.broadcast_to(f(inp["final_norm"]).reshape(1, D), (128, D))
    vec = cstf[:, C_VEC:]
    vec[:, V_LNG:V_LNG + 4] = f(inp["even_sgu_ln_g"])[0].reshape(4, 128).T
    vec[:, V_LNB:V_LNB + 4] = f(inp["even_sgu_ln_b"])[0].reshape(4, 128).T
    vec[:, V_PB:V_PB + 4] = f(inp["even_pool_b"])[0].T
    vec[:, V_PS:V_PS + 4] = f(inp["even_pool_scale"])[0].reshape(4, 128).T
    cw = f(inp["odd_conv_w"])[0]
    vec[:, V_CW:V_CW + 24] = cw.reshape(3, 8, 128).transpose(2, 0, 1).reshape(128, 24)
    vec[:, V_CB:V_CB + 8] = f(inp["odd_conv_b"])[0].reshape(8, 128).T
    vec[:, V_N_EVEN:V_N_EVEN + 8] = f(inp["even_norm"])[0].reshape(8, 128).T
    vec[:, V_N_ODD:V_N_ODD + 8] = f(inp["odd_norm"])[0].reshape(8, 128).T
    vec[:, V_N_FFN0:V_N_FFN0 + 8] = f(inp["ffn_norm"])[0].reshape(8, 128).T
    vec[:, V_N_FFN1:V_N_FFN1 + 8] = f(inp["ffn_norm"])[1].reshape(8, 128).T
    vec[:, V_EPS] = EPS
    vec[:, V_MHALF] = -0.5
    vec[:, V_1P5] = 1.5
    vec[:, V_MAGIC] = 1597463007.0
    cstb = np.zeros((128, NCB), np.float32)
    cstb[:, CB_ID:CB_ID + 128] = np.eye(128, dtype=np.float32)
    cstb[:, CB_B:] = _pool_mats().reshape(128, 4 * 3 * 128)
    wst = np.ascontiguousarray(f(inp["even_sgu_ws"])[0].transpose(2, 0, 1))
    pw = np.ascontiguousarray(f(inp["even_pool_w"])[0].transpose(1, 0, 2))
    c = lambda a: np.ascontiguousarray(f(a))
    return {
        "cstf": cstf, "cstb": cstb, "wst": wst, "pw": pw,
        "win0": c(inp["even_w_in"][0]), "wout0": c(inp["even_w_out"][0]),
        "win1": c(inp["odd_w_in"][0]), "wout1": c(inp["odd_w_out"][0]),
        "wg0": c(inp["ffn_w_gate"][0]), "wg1": c(inp["ffn_w_gate"][1]),
        "wu0": c(inp["ffn_w_up"][0]), "wu1": c(inp["ffn_w_up"][1]),
        "wd0": c(inp["ffn_w_down"][0]), "wd1": c(inp["ffn_w_down"][1]),
    }


_NC_CACHE = {}


def _get_nc(layers, final):
    key = (tuple(layers), final)
    if key not in _NC_CACHE:
        _NC_CACHE[key] = build_nc(layers=layers, final=final)
    return _NC_CACHE[key]


def _launch(xs, consts, layers, final):
    nc = _get_nc(layers, final)
    in_maps = [dict(consts, x=xs[c]) for c in range(NCORES)]
    res = run_bass_kernel_spmd(nc, in_maps, core_ids=list(range(NCORES)))
    return [np.asarray(res.results[c]["y"]) for c in range(NCORES)]


def kernel(**inputs):
    x = np.asarray(inputs["x"], np.float32)
    consts = _prep_consts(inputs)
    xs = [np.ascontiguousarray(x[2 * c:2 * c + 2].reshape(TOK_CORE, D)) for c in range(NCORES)]
    ys = _launch(xs, consts, (0, 1), True)
    out = np.stack([y.reshape(2, SEQ, D) for y in ys], axis=0).reshape(16, SEQ, D)
    return out.astype(np.float32)
```
